# Optimizing a Trainium2 kernel written in Bass

```python
import math
import jax
import jax.numpy as jnp
from jax import lax
import numpy as np

D_MODEL = 1024
BATCH = 16
SEQ = 4096
DEPTH = 4
DEC_BATCH = 4
DEC_SEQ = 8192
PAST_LEN = 128

GRID_W = 64
Q_BLOCK = 128
N_MIXERS = 3
NORM_EPS = 1e-6

MLA_HEADS = 16
MLA_Q_LORA = 256
MLA_KV_LORA = 128
MLA_NOPE = 64
MLA_ROPE = 32
MLA_V = 64
MLA_WIDTH = MLA_HEADS * MLA_V
MLA_IN_WIDTH = MLA_Q_LORA + MLA_KV_LORA + MLA_ROPE + MLA_WIDTH
MLA_ROPE_THETA = 10000.0

GQA_HEADS = 16
GQA_KV_HEADS = 4
GQA_GROUP = GQA_HEADS // GQA_KV_HEADS
GQA_HD = 64
GQA_WIDTH = GQA_HEADS * GQA_HD
GQA_KV_WIDTH = GQA_KV_HEADS * GQA_HD
GQA_IN_WIDTH = GQA_WIDTH + 2 * GQA_KV_WIDTH + GQA_WIDTH
GQA_ROPE_THETA = 10000.0

DIFF_HEADS = 8
DIFF_HD = 64
DIFF_QK_WIDTH = DIFF_HEADS * 2 * DIFF_HD
DIFF_WIDTH = DIFF_HEADS * 2 * DIFF_HD
DIFF_IN_WIDTH = 2 * DIFF_QK_WIDTH + 2 * DIFF_WIDTH
DIFF_LAMBDA_STD = 0.1

kernel_name = "hybrid_mla_gqa_diff_encoder"


def rms_norm(x, g, eps=NORM_EPS):
    xf = x.astype(jnp.float32)
    y = xf * lax.rsqrt(jnp.mean(xf * xf, axis=-1, keepdims=True) + eps)
    return (y * g.astype(jnp.float32)).astype(x.dtype)


def rope_angles(pos, dim, theta):
    inv = 1.0 / (theta ** (jnp.arange(0, dim, 2, dtype=jnp.float32) / dim))
    ang = pos[:, None] * inv[None, :]
    return jnp.cos(ang), jnp.sin(ang)


def apply_rope(x, cos, sin):
    d2 = x.shape[-1] // 2
    x1, x2 = x[..., :d2], x[..., d2:]
    c = cos[:, None, :]
    s = sin[:, None, :]
    return jnp.concatenate([x1 * c - x2 * s, x1 * s + x2 * c], axis=-1).astype(x.dtype)


def alibi_slopes(n):
    return jnp.exp2(-8.0 * jnp.arange(1, n + 1, dtype=jnp.float32) / n)


def sweep_query_blocks(fn, *q_arrays):
    B, S = q_arrays[0].shape[:2]
    nb = S // Q_BLOCK
    blocked = [jnp.moveaxis(a.reshape((B, nb, Q_BLOCK) + a.shape[2:]), 1, 0) for a in q_arrays]
    out = lax.map(lambda args: fn(args[0], *args[1]), (jnp.arange(nb), blocked))
    return jnp.moveaxis(out, 0, 1).reshape((B, S) + out.shape[3:])


def mla_mixer(h, p):
    B, S, _ = h.shape
    f32 = jnp.float32
    proj = h @ p["w_in"]
    c_q, c_kv, k_rope, gate = jnp.split(
        proj, [MLA_Q_LORA, MLA_Q_LORA + MLA_KV_LORA, MLA_Q_LORA + MLA_KV_LORA + MLA_ROPE], axis=-1)
    q = (rms_norm(c_q, p["q_norm"]) @ p["w_uq"]).reshape(B, S, MLA_HEADS, MLA_NOPE + MLA_ROPE)
    kv = (rms_norm(c_kv, p["kv_norm"]) @ p["w_ukv"]).reshape(B, S, MLA_HEADS, MLA_NOPE + MLA_V)
    q_nope, q_rope = q[..., :MLA_NOPE], q[..., MLA_NOPE:]
    k_nope, v = kv[..., :MLA_NOPE], kv[..., MLA_NOPE:]
    cos, sin = rope_angles(jnp.arange(S, dtype=f32), MLA_ROPE, MLA_ROPE_THETA)
    q_rope = apply_rope(q_rope, cos, sin)
    k_rope = apply_rope(k_rope[:, :, None, :], cos, sin)[:, :, 0]
    scale = (MLA_NOPE + MLA_ROPE) ** -0.5

    def block(i, qn, qr):
        s = (jnp.einsum("bqhd,bkhd->bhqk", qn, k_nope)
             + jnp.einsum("bqhr,bkr->bhqk", qr, k_rope)).astype(f32) * scale
        a = jax.nn.softmax(s, axis=-1).astype(v.dtype)
        return jnp.einsum("bhqk,bkhd->bqhd", a, v).reshape(B, Q_BLOCK, MLA_WIDTH)

    o = sweep_query_blocks(block, q_nope, q_rope)
    return (o * jax.nn.silu(gate)) @ p["w_out"]


def gqa_mixer(h, p):
    B, S, _ = h.shape
    f32 = jnp.float32
    proj = h @ p["w_in"]
    q, k, v, gate = jnp.split(
        proj, [GQA_WIDTH, GQA_WIDTH + GQA_KV_WIDTH, GQA_WIDTH + 2 * GQA_KV_WIDTH], axis=-1)
    q = rms_norm(q.reshape(B, S, GQA_HEADS, GQA_HD), p["q_norm"])
    k = rms_norm(k.reshape(B, S, GQA_KV_HEADS, GQA_HD), p["k_norm"])
    v = v.reshape(B, S, GQA_KV_HEADS, GQA_HD)
    rows = S // GRID_W
    row_pos = jnp.broadcast_to(jnp.arange(rows, dtype=f32)[:, None], (rows, GRID_W)).reshape(S)
    col_pos = jnp.broadcast_to(jnp.arange(GRID_W, dtype=f32)[None, :], (rows, GRID_W)).reshape(S)
    half = GQA_HD // 2
    cr, sr = rope_angles(row_pos, half, GQA_ROPE_THETA)
    cc, sc = rope_angles(col_pos, half, GQA_ROPE_THETA)

    def axial(x):
        return jnp.concatenate([apply_rope(x[..., :half], cr, sr),
                                apply_rope(x[..., half:], cc, sc)], axis=-1)

    q = axial(q).reshape(B, S, GQA_KV_HEADS, GQA_GROUP, GQA_HD)
    k = axial(k)
    scale = GQA_HD ** -0.5

    def block(i, qb):
        s = jnp.einsum("bqkgd,bskd->bkgqs", qb, k).astype(f32) * scale
        a = jax.nn.softmax(s, axis=-1).astype(v.dtype)
        return jnp.einsum("bkgqs,bskd->bqkgd", a, v).reshape(B, Q_BLOCK, GQA_WIDTH)

    o = sweep_query_blocks(block, q)
    return (o * jax.nn.silu(gate)) @ p["w_out"]


def diff_mixer(h, p, layer_idx):
    B, S, _ = h.shape
    f32 = jnp.float32
    lambda_init = 0.8 - 0.6 * math.exp(-0.3 * layer_idx)
    proj = h @ p["w_in"]
    q, k, v, gate = jnp.split(
        proj, [DIFF_QK_WIDTH, 2 * DIFF_QK_WIDTH, 2 * DIFF_QK_WIDTH + DIFF_WIDTH], axis=-1)
    q = q.reshape(B, S, DIFF_HEADS, 2, DIFF_HD)
    k = k.reshape(B, S, DIFF_HEADS, 2, DIFF_HD)
    v = v.reshape(B, S, DIFF_HEADS, 2 * DIFF_HD)
    lam = (jnp.exp(jnp.sum(p["lambda_q1"].astype(f32) * p["lambda_k1"].astype(f32)))
           - jnp.exp(jnp.sum(p["lambda_q2"].astype(f32) * p["lambda_k2"].astype(f32)))
           + lambda_init)
    slopes = alibi_slopes(DIFF_HEADS)
    key_pos = jnp.arange(S, dtype=f32)
    scale = DIFF_HD ** -0.5

    def block(i, qb):
        q_pos = (i * Q_BLOCK + jnp.arange(Q_BLOCK)).astype(f32)
        bias = -slopes[:, None, None] * jnp.abs(q_pos[:, None] - key_pos[None, :])[None]
        s = jnp.einsum("bqhcd,bkhcd->bchqk", qb, k).astype(f32) * scale + bias
        a = jax.nn.softmax(s, axis=-1)
        attn = a[:, 0] - lam * a[:, 1]
        o = jnp.einsum("bhqk,bkhd->bqhd", attn.astype(v.dtype), v)
        o = rms_norm(o, p["subln"]) * (1.0 - lambda_init)
        return o.reshape(B, Q_BLOCK, DIFF_WIDTH)

    o = sweep_query_blocks(block, q)
    return (o * jax.nn.silu(gate)) @ p["w_out"]


def setup_inputs(seed: int = 0) -> dict:
    key = jax.random.key(seed)
    k_xp, k_xs, k_layers = jax.random.split(key, 3)
    layer_keys = jax.random.split(k_layers, DEPTH)

    def dense(k, fan_in, fan_out):
        return jax.random.normal(k, (fan_in, fan_out), jnp.float32) * fan_in ** -0.5

    def gain(k, n):
        return 1.0 + 0.02 * jax.random.normal(k, (n,), jnp.float32)

    inputs = {
        "x_prompt": jax.random.normal(k_xp, (BATCH, SEQ, D_MODEL), jnp.float32),
        "x_sample": jax.random.normal(k_xs, (DEC_BATCH, DEC_SEQ, D_MODEL), jnp.float32),
    }
    for i in range(DEPTH):
        ks = jax.random.split(layer_keys[i], 10)
        pre = "l%d_" % i
        kind = i % N_MIXERS
        inputs[pre + "pre_norm"] = gain(ks[0], D_MODEL)
        if kind == 0:
            inputs[pre + "w_in"] = dense(ks[1], D_MODEL, MLA_IN_WIDTH)
            inputs[pre + "q_norm"] = gain(ks[2], MLA_Q_LORA)
            inputs[pre + "w_uq"] = dense(ks[3], MLA_Q_LORA, MLA_HEADS * (MLA_NOPE + MLA_ROPE))
            inputs[pre + "kv_norm"] = gain(ks[4], MLA_KV_LORA)
            inputs[pre + "w_ukv"] = dense(ks[5], MLA_KV_LORA, MLA_HEADS * (MLA_NOPE + MLA_V))
            inputs[pre + "w_out"] = dense(ks[6], MLA_WIDTH, D_MODEL)
        elif kind == 1:
            inputs[pre + "w_in"] = dense(ks[1], D_MODEL, GQA_IN_WIDTH)
            inputs[pre + "q_norm"] = gain(ks[2], GQA_HD)
            inputs[pre + "k_norm"] = gain(ks[3], GQA_HD)
            inputs[pre + "w_out"] = dense(ks[4], GQA_WIDTH, D_MODEL)
        else:
            inputs[pre + "w_in"] = dense(ks[1], D_MODEL, DIFF_IN_WIDTH)
            inputs[pre + "lambda_q1"] = DIFF_LAMBDA_STD * jax.random.normal(ks[2], (DIFF_HD,), jnp.float32)
            inputs[pre + "lambda_k1"] = DIFF_LAMBDA_STD * jax.random.normal(ks[3], (DIFF_HD,), jnp.float32)
            inputs[pre + "lambda_q2"] = DIFF_LAMBDA_STD * jax.random.normal(ks[4], (DIFF_HD,), jnp.float32)
            inputs[pre + "lambda_k2"] = DIFF_LAMBDA_STD * jax.random.normal(ks[5], (DIFF_HD,), jnp.float32)
            inputs[pre + "subln"] = gain(ks[6], 2 * DIFF_HD)
            inputs[pre + "w_out"] = dense(ks[7], DIFF_WIDTH, D_MODEL)
        inputs[pre + "post_norm"] = gain(ks[9], D_MODEL)
    return inputs


def reference(x_prompt, x_sample,
              l0_pre_norm, l0_w_in, l0_q_norm, l0_w_uq, l0_kv_norm, l0_w_ukv, l0_w_out, l0_post_norm,
              l1_pre_norm, l1_w_in, l1_q_norm, l1_k_norm, l1_w_out, l1_post_norm,
              l2_pre_norm, l2_w_in, l2_lambda_q1, l2_lambda_k1, l2_lambda_q2, l2_lambda_k2, l2_subln,
              l2_w_out, l2_post_norm,
              l3_pre_norm, l3_w_in, l3_q_norm, l3_w_uq, l3_kv_norm, l3_w_ukv, l3_w_out, l3_post_norm):
    layers = [
        dict(pre_norm=l0_pre_norm, w_in=l0_w_in, q_norm=l0_q_norm, w_uq=l0_w_uq,
             kv_norm=l0_kv_norm, w_ukv=l0_w_ukv, w_out=l0_w_out, post_norm=l0_post_norm),
        dict(pre_norm=l1_pre_norm, w_in=l1_w_in, q_norm=l1_q_norm, k_norm=l1_k_norm,
             w_out=l1_w_out, post_norm=l1_post_norm),
        dict(pre_norm=l2_pre_norm, w_in=l2_w_in, lambda_q1=l2_lambda_q1, lambda_k1=l2_lambda_k1,
             lambda_q2=l2_lambda_q2, lambda_k2=l2_lambda_k2, subln=l2_subln,
             w_out=l2_w_out, post_norm=l2_post_norm),
        dict(pre_norm=l3_pre_norm, w_in=l3_w_in, q_norm=l3_q_norm, w_uq=l3_w_uq,
             kv_norm=l3_kv_norm, w_ukv=l3_w_ukv, w_out=l3_w_out, post_norm=l3_post_norm),
    ]

    def trunk(x):
        for i in range(DEPTH):
            p = layers[i]
            kind = i % N_MIXERS
            h = rms_norm(x, p["pre_norm"])
            if kind == 0:
                m = mla_mixer(h, p)
            elif kind == 1:
                m = gqa_mixer(h, p)
            else:
                m = diff_mixer(h, p, i)
            x = x + rms_norm(m, p["post_norm"])
        return x

    y_prompt = trunk(x_prompt)
    y_sample = trunk(x_sample)
    return (y_prompt, y_sample)
```

```python
import math
from contextlib import ExitStack

import numpy as np
import ml_dtypes

import concourse.bass as bass
import concourse.mybir as mybir
from concourse.bass_utils import run_bass_kernel_spmd

F32 = mybir.dt.float32
BF16 = mybir.dt.bfloat16
AF = mybir.ActivationFunctionType
ALU = mybir.AluOpType
NPBF = ml_dtypes.bfloat16

import os
PADV = int(os.environ.get('PADV', '1'))
PADK = int(os.environ.get('PADK', '1'))
DEFER = int(os.environ.get('DEFER', '6'))
TRACE = int(os.environ.get('KTRACE', '0'))
D = 1024
EPS = 1e-6
ENGS = ["pe", "act", "dve", "pool", "sp"]


class Op:
    __slots__ = ("fn", "deps", "ref", "dma", "milestone")

    def __init__(self, fn, deps, ref, dma):
        self.fn, self.deps, self.ref, self.dma, self.milestone = fn, deps, ref, dma, False


class Prog:
    def __init__(self, nc, stack):
        self.nc = nc
        self.stack = stack
        self.ops = {e: [] for e in ENGS}
        self.last_w = {}
        self.readers = {}
        self.dma_cnt = {}
        self.dma_sem = {}
        self.dma_slot = {}
        self.eng_sem = {e: stack.enter_context(nc.semaphore("sem_" + e)) for e in ENGS}

    def op(self, eng, fn, r=(), w=(), dma=None):
        if dma is not None:
            if dma not in self.dma_slot:
                self.dma_slot[dma] = "slot%03d" % len(self.dma_slot)
            dma = self.dma_slot[dma]
            n = self.dma_cnt.get(dma, 0) + 1
            self.dma_cnt[dma] = n
            if dma not in self.dma_sem:
                self.dma_sem[dma] = self.stack.enter_context(self.nc.semaphore("dq_" + dma))
            ref = ("d", dma, n)
        else:
            ref = ("c", eng, len(self.ops[eng]))
        deps = set()
        for k in r:
            lw = self.last_w.get(k)
            if lw is not None:
                deps.add(lw)
        for k in w:
            lw = self.last_w.get(k)
            if lw is not None:
                deps.add(lw)
            rd = self.readers.get(k)
            if rd:
                deps.update(rd.values())
        for k in r:
            self.readers.setdefault(k, {})[(ref[0], ref[1])] = ref
        for k in w:
            self.last_w[k] = ref
            self.readers[k] = {}
        if dma is None and eng == "pe":
            deps = {d for d in deps if not (d[0] == "c" and d[1] == "pe")}
        deps.discard(ref)
        self.ops[eng].append(Op(fn, deps, ref, dma))

    def barrier(self):
        deps = set()
        for e in ENGS:
            for o in reversed(self.ops[e]):
                if o.dma is None and o.fn is not None:
                    deps.add(o.ref)
                    break
        for k, n in self.dma_cnt.items():
            deps.add(("d", k, n))
        for e in ENGS:
            self.ops[e].append(Op(None, set(deps), ("c", e, len(self.ops[e])), None))
        self.last_w = {}
        self.readers = {}
        self.dma_slot = {}

    def I(self, eng, name, r=(), w=(), dma=None, **kw):
        self.op(eng, (name, kw), r=r, w=w, dma=dma)

    def emit(self):
        for e in ENGS:
            for o in self.ops[e]:
                for d in o.deps:
                    if d[0] == "c":
                        self.ops[d[1]][d[2]].milestone = True
        rank = {}
        for e in ENGS:
            c = 0
            rk = []
            for o in self.ops[e]:
                if o.milestone:
                    c += 1
                rk.append(c)
            rank[e] = rk

        def run(ename, e):
            known = {}
            for o in self.ops[ename]:
                for d in sorted(o.deps):
                    if d[0] == "c":
                        sem, val, key = self.eng_sem[d[1]], rank[d[1]][d[2]], ("c", d[1])
                    else:
                        sem, val, key = self.dma_sem[d[1]], 16 * d[2], ("d", d[1])
                    if known.get(key, 0) < val:
                        e.wait_ge(sem, val)
                        known[key] = val
                if o.fn is None:
                    continue
                ins = getattr(e, o.fn[0])(**o.fn[1])
                if o.dma is not None:
                    ins.then_inc(self.dma_sem[o.dma], 16)
                elif o.milestone:
                    ins.then_inc(self.eng_sem[ename], 1)

        with self.nc.Block() as block:
            block.tensor(lambda e: run("pe", e))
            block.scalar(lambda e: run("act", e))
            block.vector(lambda e: run("dve", e))
            block.gpsimd(lambda e: run("pool", e))
            block.sync(lambda e: run("sp", e))


class Arena:
    def __init__(self, ap, words):
        self.ap, self.words, self.off, self.base = ap, words, 0, 0

    def mark(self):
        self.base = self.off

    def reset(self):
        self.off = self.base

    def alloc(self, shape, dtype):
        p = shape[0]
        n = 1
        for s in shape[1:]:
            n *= s
        esz = 4 if dtype == F32 else 2
        words = (n * esz + 3) // 4
        words = (words + 7) // 8 * 8
        assert self.off + words <= self.words, ("SBUF arena overflow", self.off, words, self.words)
        v = self.ap[0:p, self.off:self.off + words]
        self.off += words
        if dtype != F32:
            v = v.bitcast(dtype)
        v = v[:, 0:n]
        if len(shape) == 3:
            v = v.rearrange("p (a b) -> p a b", b=shape[2])
        elif len(shape) == 4:
            v = v.rearrange("p (a b c) -> p a b c", b=shape[2], c=shape[3])
        return v


class Buf:
    _n = 0

    def __init__(self, ap, name=None):
        Buf._n += 1
        self.ap = ap
        self.key = (name or "buf") + "#" + str(Buf._n)


class Builder:
    def __init__(self, slot, layers):
        self.SLOT = slot
        self.T = 4 * slot
        self.NQB = slot // 512
        self.NKB = slot // 128
        self.layers = layers
        self.nc = bass.Bass("TRN2", target_bir_lowering=False)
        self.inputs = {}

    def din(self, name, shape, dtype=F32):
        t = self.nc.dram_tensor(name, list(shape), dtype, kind="ExternalInput").ap()
        self.inputs[name] = (tuple(shape), dtype)
        return t

    def dscr(self, name, shape, dtype):
        return self.nc.dram_tensor(name, list(shape), dtype, kind="Internal").ap()

    def rr(self, lst):
        self._rr = getattr(self, "_rr", 0) + 1
        return lst[self._rr % len(lst)]

    def build(self):
        nc = self.nc
        T, SLOT = self.T, self.SLOT
        self.x_in = self.din("xin", [4, SLOT, D])
        self.bsel_d = self.din("bsel", [128, 2])
        self.ident_d = self.din("ident", [128, 128], BF16)
        self.y_out = nc.dram_tensor("y", [4, SLOT, D], F32, kind="ExternalOutput").ap()
        self.xs = [self.dscr("xs0", [4, SLOT, D], F32), self.dscr("xs1", [4, SLOT, D], F32)]
        self.QS = self.dscr("QS", [D + 512, T], BF16)
        self.KS = self.dscr("KS", [D + 32, T], BF16)
        self.VS = self.dscr("VS", [T, D], BF16)
        self.GS = self.dscr("GS", [D, T], BF16)
        self.OS = self.dscr("OS", [D, T], BF16)
        self.lw = []
        for li, (kind, ridx) in enumerate(self.layers):
            self.lw.append(self.declare_layer(li, kind))

        with ExitStack() as stack:
            WORDS = 47 * 1024
            arena_t = stack.enter_context(nc.sbuf_tensor("arena", [128, WORDS], F32))
            self.A = Arena(arena_t[:], WORDS)
            self.psum = [stack.enter_context(nc.psum_tensor("psb%d" % i, [128, 1024], F32)) for i in range(4)]
            self.P = Prog(nc, stack)
            self.consts()
            self.A.mark()
            nl = len(self.layers)
            for li, (kind, ridx) in enumerate(self.layers):
                xin = self.x_in if li == 0 else self.xs[li % 2]
                xout = self.y_out if li == nl - 1 else self.xs[(li + 1) % 2]
                W = self.lw[li]
                self.A.reset()
                getattr(self, "phaseA_" + kind)(W, xin, ridx)
                self.P.barrier()
                self.A.reset()
                getattr(self, "phaseB_" + kind)(W, ridx)
                self.P.barrier()
                self.A.reset()
                self.phaseC(W, xin, xout)
                self.P.barrier()
            self.P.emit()
        return nc

    def bank(self, k):
        return self.psum[k // 2][:, (k % 2) * 512:(k % 2) * 512 + 512]

    def bank2(self, k):
        return self.psum[k // 2][:, :]

    def consts(self):
        P, A = self.P, self.A
        self.ident = Buf(A.alloc([128, 128], BF16), "ident")
        self.bsel = Buf(A.alloc([128, 2], F32), "bsel")
        self.ones_b = Buf(A.alloc([128, 128], BF16), "ones_b")
        self.ones_f = Buf(A.alloc([128, 64], F32), "ones_f")
        self.epsc = Buf(A.alloc([128, 1], F32), "epsc")
        P.I("sp", "dma_start", w=[self.ident.key], dma="c_ident", out=self.ident.ap, in_=self.ident_d)
        P.I("sp", "dma_start", w=[self.bsel.key], dma="c_bsel", out=self.bsel.ap, in_=self.bsel_d)
        P.I("pool", "memset", w=[self.ones_b.key], ap=self.ones_b.ap, constant=1.0)
        P.I("pool", "memset", w=[self.ones_f.key], ap=self.ones_f.ap, constant=1.0)
        P.I("pool", "memset", w=[self.epsc.key], ap=self.epsc.ap, constant=EPS)

    def psk(self, *banks):
        return [("ps", k) for k in banks]

    def rsqrt(self, out_b, out_ap, in_ap, in_keys, scale, tmp_b, tmp_ap):
        P = self.P
        np_ = out_ap.shape[0]
        P.I("act", "activation", r=[k for k in in_keys if not isinstance(k, tuple)] + [self.epsc.key],
            w=[tmp_b.key] + [k for k in in_keys if isinstance(k, tuple)],
            out=tmp_ap, in_=in_ap, func=AF.Ln, scale=scale, bias=self.epsc.ap[0:np_, :])
        P.I("act", "activation", r=[tmp_b.key], w=[out_b.key], out=out_ap, in_=tmp_ap, func=AF.Exp, scale=-0.5)

    def evac(self, eng, dst_b, dst_ap, bank, src_ap):
        if eng == "dve":
            self.P.I("dve", "tensor_copy", w=[dst_b.key, ("ps", bank)], out=dst_ap, in_=src_ap)
        else:
            self.P.I("act", "activation", w=[dst_b.key, ("ps", bank)], out=dst_ap, in_=src_ap, func=AF.Copy)

    def load_weight(self, dst_b, dst_ap, src_ap, gain_ap, stage_b, rows):
        P = self.P
        n = src_ap.shape[1]
        st = stage_b.ap[0:rows, 0:n]
        P.I("sp", "dma_start", w=[stage_b.key], dma=stage_b.key, out=st, in_=src_ap)
        if gain_ap is None:
            P.I("dve", "tensor_copy", r=[stage_b.key], w=[dst_b.key], out=dst_ap, in_=st)
        else:
            P.I("dve", "tensor_scalar", r=[stage_b.key, self.gains.key], w=[dst_b.key],
                out=dst_ap, in0=st, scalar1=gain_ap, scalar2=None, op0=ALU.mult)

    def prenorm_bufs(self):
        A = self.A
        return {
            "xt": [Buf(A.alloc([128, 4, D], F32), "xt") for _ in range(2)],
            "hb": Buf(A.alloc([128, 4, D], BF16), "hb"),
            "hT": [Buf(A.alloc([128, 8, 512], BF16), "hT") for _ in range(2)],
            "ss": [Buf(A.alloc([128, 4], F32), "ss") for _ in range(2)],
            "rs": [Buf(A.alloc([128, 4], F32), "rs") for _ in range(2)],
            "tmp4": [Buf(A.alloc([128, 4], F32), "tmp4") for _ in range(2)],
            "junk": Buf(A.alloc([128, D], BF16), "junk"),
        }

    def prenorm_tile(self, xin, t, bufs):
        P = self.P
        SLOT = self.SLOT
        slot, tt = divmod(t, SLOT // 512)
        xt, hb, hT = bufs["xt"][t % 2], bufs["hb"], bufs["hT"][t % 2]
        ss, rs, tmp, junk = bufs["ss"][t % 2], bufs["rs"][t % 2], bufs["tmp4"][t % 2], bufs["junk"]
        src = xin[slot, tt * 512:(tt + 1) * 512, :].rearrange("(s p) d -> p s d", p=128)
        P.I("sp", "dma_start", w=[xt.key], dma=xt.key, out=xt.ap, in_=src)
        for s in range(4):
            P.I("act", "activation", r=[xt.key], w=[junk.key, ss.key], out=junk.ap, in_=xt.ap[:, s, :],
                func=AF.Square, accum_out=ss.ap[:, s:s + 1])
        self.rsqrt(rs, rs.ap, ss.ap, [ss.key], 1.0 / D, tmp, tmp.ap)
        for s in range(4):
            if s % 2 == 0:
                P.I("dve", "tensor_scalar", r=[xt.key, rs.key], w=[hb.key], out=hb.ap[:, s, :], in0=xt.ap[:, s, :],
                    scalar1=rs.ap[:, s:s + 1], scalar2=None, op0=ALU.mult)
            else:
                P.I("pool", "tensor_scalar", r=[xt.key, rs.key], w=[hb.key], out=hb.ap[:, s, :], in0=xt.ap[:, s, :],
                    scalar1=rs.ap[:, s:s + 1], scalar2=0.0, op0=ALU.mult, op1=ALU.add)
        for cp in range(4):
            bk = cp % 2
            psv = self.bank(bk).bitcast(BF16)
            for ci in range(2):
                c = cp * 2 + ci
                for s in range(4):
                    P.I("pe", "transpose", r=[hb.key, self.ident.key], w=[("ps", bk)],
                        out=psv[:, ci * 512 + s * 128: ci * 512 + (s + 1) * 128],
                        in_=hb.ap[:, s, c * 128:(c + 1) * 128], identity=self.ident.ap)
            self.evac("dve" if cp % 2 == 0 else "act", hT, hT.ap[:, cp * 2:cp * 2 + 2, :], bk,
                      psv.rearrange("p (a b) -> p a b", b=512))
        return hT

    def proj_fm(self, bk, hT, w_ap_fn, nk, M, w_keys):
        for c in range(nk):
            self.P.I("pe", "matmul", r=[hT.key] + list(w_keys), w=[("ps", bk)],
                     out=self.bank(bk)[0:M, :], lhsT=w_ap_fn(c), rhs=hT.ap[:, c, :], start=(c == 0), stop=(c == nk - 1))

    def declare_layer(self, li, kind):
        pre = "l%d_" % li
        W = {"kind": kind}
        W["gpre"] = self.din(pre + "gpre", [128, 8])
        W["gpost"] = self.din(pre + "gpost", [1, D])
        W["wout"] = self.din(pre + "wout", [D, D])
        if kind == "mla":
            W["win"] = self.din(pre + "win", [D, 1472])
            W["wuq"] = self.din(pre + "wuq", [256, 3072])
            W["wuk"] = self.din(pre + "wuk", [128, 1024])
            W["wuv"] = self.din(pre + "wuv", [128, 1024])
            W["glat"] = self.din(pre + "glat", [128, 3])
            W["rope"] = self.din(pre + "rope", [2, 32, self.T])
        elif kind == "gqa":
            W["win"] = self.din(pre + "win", [D, 3840])
            W["gqk"] = self.din(pre + "gqk", [128, 4])
            W["rope"] = self.din(pre + "rope", [2, 128, self.T])
        elif kind == "diff":
            W["win"] = self.din(pre + "win", [D, 4096])
            W["lamv"] = self.din(pre + "lamv", [64, 4])
            W["subln"] = self.din(pre + "subln", [64, 2])
            W["urows"] = self.din(pre + "urows", [2, 5, self.T], BF16)
            W["wrows"] = self.din(pre + "wrows", [8, 5, self.T], BF16)
            W["absd"] = self.din(pre + "absd", [4, 128, 2048])
        return W

    def phaseA_mla(self, W, xin, ridx):
        P, A = self.P, self.A
        T = self.T
        self.gains = Buf(A.alloc([128, 12], F32), "gains")
        P.I("sp", "dma_start", w=[self.gains.key], dma="gains", out=self.gains.ap[:, 0:8], in_=W["gpre"])
        P.I("sp", "dma_start", w=[self.gains.key], dma="gains", out=self.gains.ap[:, 8:11], in_=W["glat"])
        stage = Buf(A.alloc([128, 3072], F32), "stage")
        win = Buf(A.alloc([128, 8, 1472], BF16), "win")
        wuq = Buf(A.alloc([128, 2, 3072], BF16), "wuq")
        wuk = Buf(A.alloc([128, 1024], BF16), "wuk")
        wuv = Buf(A.alloc([128, 1024], BF16), "wuv")
        gn = self.gains.ap
        for c in range(8):
            self.load_weight(win, win.ap[:, c, :], W["win"][c * 128:(c + 1) * 128, :], gn[:, c:c + 1], stage, 128)
        for c in range(2):
            self.load_weight(wuq, wuq.ap[:, c, :], W["wuq"][c * 128:(c + 1) * 128, :], gn[:, 8 + c:9 + c], stage, 128)
        self.load_weight(wuk, wuk.ap, W["wuk"], gn[:, 10:11], stage, 128)
        self.load_weight(wuv, wuv.ap, W["wuv"], gn[:, 10:11], stage, 128)
        pb = self.prenorm_bufs()
        rope = [Buf(A.alloc([96, 2, 512], F32), "rope") for _ in range(2)]
        sq = [Buf(A.alloc([128, 512], BF16), "sq") for _ in range(3)]
        rstd = Buf(A.alloc([128, 512], F32), "rstd")
        rtmp = Buf(A.alloc([128, 512], F32), "rtmp")
        latn = [Buf(A.alloc([128, 3, 512], BF16), "latn") for _ in range(2)]
        krt = [Buf(A.alloc([32, 512], F32), "krt") for _ in range(2)]
        kr = Buf(A.alloc([32, 512], BF16), "kr")
        gt = [Buf(A.alloc([128, 512], BF16), "gt") for _ in range(3)]
        qt = [Buf(A.alloc([96, 512], BF16), "qt") for _ in range(4)]
        qtmp = [Buf(A.alloc([96, 2, 512], F32), "qtmp") for _ in range(2)]
        kt = [Buf(A.alloc([64, 512], BF16), "kt") for _ in range(4)]
        vt = [Buf(A.alloc([128, 4, D], BF16), "vt") for _ in range(2)]
        obank = [2, 3, 4, 5, 6, 7]
        ob = [0]

        def nb():
            ob[0] += 1
            return obank[ob[0] % 6]

        for t in range(T // 512):
            tok = slice(t * 512, (t + 1) * 512)
            hT = self.prenorm_tile(xin, t, pb)
            rp = rope[t % 2]
            for r0 in (0, 64):
                P.I("sp", "dma_start", w=[rp.key], dma=rp.key, out=rp.ap[r0:r0 + 32, :, :], in_=W["rope"][:, :, tok].rearrange("a p t -> p a t"))
            ln = latn[t % 2]
            banks = [nb(), nb(), nb()]
            for j in range(3):
                self.proj_fm(banks[j], hT, lambda c, j=j: win.ap[:, c, j * 128:(j + 1) * 128], 8, 128, [win.key])
                P.I("act", "activation", w=[sq[j].key, ("ps", banks[j])], out=sq[j].ap, in_=self.bank(banks[j]), func=AF.Square)
            for js, n in (([0, 1], 256.0), ([2], 128.0)):
                sb = nb()
                for i, j in enumerate(js):
                    P.I("pe", "matmul", r=[self.ones_b.key, sq[j].key], w=[("ps", sb)], out=self.bank(sb),
                        lhsT=self.ones_b.ap, rhs=sq[j].ap, start=(i == 0), stop=(i == len(js) - 1))
                self.rsqrt(rstd, rstd.ap, self.bank(sb), [("ps", sb)], 1.0 / n, rtmp, rtmp.ap)
                for j in js:
                    P.I("dve", "tensor_tensor", r=[rstd.key], w=[ln.key, ("ps", banks[j])], out=ln.ap[:, j, :],
                        in0=self.bank(banks[j]), in1=rstd.ap, op=ALU.mult)
            ba, bb = nb(), nb()
            self.proj_fm(ba, hT, lambda c: win.ap[:, c, 384:416], 8, 32, [win.key])
            self.proj_fm(bb, hT, lambda c: win.ap[:, c, 416:448], 8, 32, [win.key])
            k1, k2 = krt[0], krt[1]
            P.I("dve", "tensor_tensor", r=[rp.key], w=[k1.key, ("ps", ba)], out=k1.ap, in0=self.bank(ba)[0:32, :],
                in1=rp.ap[0:32, 0, :], op=ALU.mult)
            P.I("dve", "tensor_tensor", r=[rp.key], w=[k2.key, ("ps", bb)], out=k2.ap, in0=self.bank(bb)[0:32, :],
                in1=rp.ap[0:32, 1, :], op=ALU.mult)
            P.I("pool", "tensor_tensor", r=[k1.key, k2.key], w=[kr.key], out=kr.ap, in0=k1.ap, in1=k2.ap, op=ALU.add)
            P.I("pool", "dma_start", r=[kr.key], dma=kr.key, out=self.KS[D:D + 32, tok], in_=kr.ap)
            for g in range(8):
                bg = nb()
                gb = gt[g % 3]
                self.proj_fm(bg, hT, lambda c, g=g: win.ap[:, c, 448 + g * 128: 448 + (g + 1) * 128], 8, 128, [win.key])
                P.I("act", "activation", w=[gb.key, ("ps", bg)], out=gb.ap, in_=self.bank(bg), func=AF.Silu)
                P.I("pool", "dma_start", r=[gb.key], dma=gb.key, out=self.GS[g * 128:(g + 1) * 128, tok], in_=gb.ap)
            for h in range(16):
                ba, bb = nb(), nb()
                for c in range(2):
                    P.I("pe", "matmul", r=[wuq.key, ln.key], w=[("ps", ba)], out=self.bank(ba)[0:96, :],
                        lhsT=wuq.ap[:, c, h * 192:h * 192 + 96], rhs=ln.ap[:, c, :], start=(c == 0), stop=(c == 1))
                for c in range(2):
                    P.I("pe", "matmul", r=[wuq.key, ln.key], w=[("ps", bb)], out=self.bank(bb)[0:96, :],
                        lhsT=wuq.ap[:, c, h * 192 + 96:h * 192 + 192], rhs=ln.ap[:, c, :], start=(c == 0), stop=(c == 1))
                qb_, qm = qt[h % 4], qtmp[h % 2]
                P.I("dve", "tensor_tensor", r=[rp.key], w=[qm.key, ("ps", ba)], out=qm.ap[64:96, 0, :],
                    in0=self.bank(ba)[64:96, :], in1=rp.ap[64:96, 0, :], op=ALU.mult)
                P.I("dve", "tensor_tensor", r=[rp.key], w=[qm.key, ("ps", bb)], out=qm.ap[64:96, 1, :],
                    in0=self.bank(bb)[64:96, :], in1=rp.ap[64:96, 1, :], op=ALU.mult)
                self.evac("act", qb_, qb_.ap[0:64, :], ba, self.bank(ba)[0:64, :])
                P.I("pool", "tensor_tensor", r=[qm.key], w=[qb_.key], out=qb_.ap[64:96, :], in0=qm.ap[64:96, 0, :],
                    in1=qm.ap[64:96, 1, :], op=ALU.add)
                P.I("pool", "dma_start", r=[qb_.key], dma=qb_.key, out=self.QS[h * 96:(h + 1) * 96, tok], in_=qb_.ap)
            for h in range(16):
                bk = nb()
                kb_ = kt[h % 4]
                P.I("pe", "matmul", r=[wuk.key, ln.key], w=[("ps", bk)], out=self.bank(bk)[0:64, :],
                    lhsT=wuk.ap[:, h * 64:(h + 1) * 64], rhs=ln.ap[:, 2, :], start=True, stop=True)
                self.evac("dve" if h % 2 == 0 else "act", kb_, kb_.ap, bk, self.bank(bk)[0:64, :])
                P.I("pool", "dma_start", r=[kb_.key], dma=kb_.key, out=self.KS[h * 64:(h + 1) * 64, tok], in_=kb_.ap)
            vb = vt[t % 2]
            for s in range(4):
                for hf in range(2):
                    bv = nb()
                    P.I("pe", "matmul", r=[wuv.key, ln.key], w=[("ps", bv)], out=self.bank(bv),
                        lhsT=ln.ap[:, 2, s * 128:(s + 1) * 128], rhs=wuv.ap[:, hf * 512:(hf + 1) * 512], start=True, stop=True)
                    self.evac("dve" if (s + hf) % 2 == 0 else "act", vb, vb.ap[:, s, hf * 512:(hf + 1) * 512], bv, self.bank(bv))
            P.I("pool", "dma_start", r=[vb.key], dma=vb.key, out=self.VS[tok, :].rearrange("(s p) d -> p s d", p=128), in_=vb.ap)

    def attn_bufs(self, dk):
        A = self.A
        SLOT, NKB = self.SLOT, self.NKB
        b = {}
        kp = 128 if PADK else dk
        b["K"] = [Buf(A.alloc([kp, SLOT], BF16), "K") for _ in range(4)]
        b["Kst"] = [Buf(A.alloc([dk, SLOT], BF16), "Kst") for _ in range(2)]
        b["V"] = [Buf(A.alloc([128, NKB + 1, 65], BF16), "V") for _ in range(4)]
        b["Vst"] = [Buf(A.alloc([128, NKB, 64], BF16), "Vst") for _ in range(2)]
        b["Q"] = [Buf(A.alloc([kp, SLOT], BF16), "Q") for _ in range(2)]
        b["G"] = [Buf(A.alloc([64, SLOT], BF16), "G") for _ in range(2)]
        if PADK:
            for t_ in b["K"] + b["Q"]:
                self.P.I("pool", "memset", w=[t_.key], ap=t_.ap, constant=0.0)
        b["PT"] = [Buf(A.alloc([128, 1024], BF16), "PT") for _ in range(3)]
        b["Xs"] = [Buf(A.alloc([65, 512], F32), "Xs") for _ in range(2)]
        b["cX"] = [Buf(A.alloc([65, 512], F32), "cX") for _ in range(2)]
        b["R"] = [Buf(A.alloc([65, 512], F32), "R") for _ in range(2)]
        b["tn"] = [Buf(A.alloc([64, 512], F32), "tn") for _ in range(2)]
        b["og"] = [Buf(A.alloc([64, 512], BF16), "og") for _ in range(2)]
        for v in b["V"]:
            self.P.I("pool", "memset", w=[v.key], ap=v.ap, constant=1.0)
        return b

    def blend(self, d2, d2ap, d3, d3ap, s2, s3):
        P = self.P
        np_ = s2.ap.shape[0]
        bcol = self.bsel.ap[0:np_, 0:1]
        nbcol = self.bsel.ap[0:np_, 1:2]
        P.I("pool", "tensor_scalar", r=[s2.key, self.bsel.key], w=[d2.key], out=d2ap, in0=s2.ap, scalar1=nbcol,
            scalar2=0.0, op0=ALU.mult, op1=ALU.add)
        P.I("pool", "tensor_scalar", r=[s3.key, self.bsel.key], w=[d3.key], out=d3ap, in0=s3.ap, scalar1=nbcol,
            scalar2=0.0, op0=ALU.mult, op1=ALU.add)
        P.I("dve", "scalar_tensor_tensor", r=[s3.key, self.bsel.key], w=[d2.key], out=d2ap, in0=s3.ap, scalar=bcol,
            in1=d2ap, op0=ALU.mult, op1=ALU.add)
        P.I("dve", "scalar_tensor_tensor", r=[s2.key, self.bsel.key], w=[d3.key], out=d3ap, in0=s2.ap, scalar=bcol,
            in1=d3ap, op0=ALU.mult, op1=ALU.add)

    def load_K(self, b, row_srcs):
        SLOT = self.SLOT
        for ks in range(4):
            dst = b["K"][ks] if ks < 2 else b["Kst"][ks - 2]
            for (r0, n, s0) in row_srcs:
                self.P.I("sp", "dma_start", w=[dst.key], dma=dst.key, out=dst.ap[r0:r0 + n, :],
                         in_=self.KS[s0:s0 + n, ks * SLOT:(ks + 1) * SLOT])
        dkk = b["Kst"][0].ap.shape[0]
        self.blend(b["K"][2], b["K"][2].ap[0:dkk, :], b["K"][3], b["K"][3].ap[0:dkk, :], b["Kst"][0], b["Kst"][1])

    def load_V(self, b, col0):
        SLOT, NKB = self.SLOT, self.NKB
        step = 8
        for ks in range(4):
            dst = b["V"][ks] if ks < 2 else b["Vst"][ks - 2]
            for k0 in range(0, NKB, step):
                src = self.VS[ks * SLOT + k0 * 128: ks * SLOT + (k0 + step) * 128, col0:col0 + 64].rearrange("(k p) d -> p k d", p=128)
                self.P.I("sp", "dma_start", w=[dst.key], dma=dst.key, out=dst.ap[:, k0:k0 + step, 0:64], in_=src)
        self.blend(b["V"][2], b["V"][2].ap[:, 0:NKB, 0:64], b["V"][3], b["V"][3].ap[:, 0:NKB, 0:64], b["Vst"][0], b["Vst"][1])

    def attention_unit(self, b, dk, scale, q_rows, g_row0, o_row0):
        P = self.P
        SLOT, NQB, NKB = self.SLOT, self.NQB, self.NKB
        NG = NKB // 2
        dkp = 128 if PADK else dk
        mv = 128 if PADV else 65
        for pair in range(2):
            for qi in range(2):
                qs = pair + 2 * qi
                Qb, Gb = b["Q"][qi], b["G"][qi]
                for (r0, n, s0) in q_rows:
                    P.I("sp", "dma_start", w=[Qb.key], dma=Qb.key, out=Qb.ap[r0:r0 + n, :],
                        in_=self.QS[s0:s0 + n, qs * SLOT:(qs + 1) * SLOT])
                P.I("sp", "dma_start", w=[Gb.key], dma=Gb.key, out=Gb.ap, in_=self.GS[g_row0:g_row0 + 64, qs * SLOT:(qs + 1) * SLOT])
            groups = [(qb, si, g) for qb in range(NQB) for si in range(2) for g in range(NG)]

            def qk(n):
                qb, si, g = groups[n]
                ks = pair + 2 * si
                sb = (n % 2) * 2
                for j in range(2):
                    kb = g * 2 + j
                    P.I("pe", "matmul", r=[b["K"][ks].key, b["Q"][si].key], w=[("ps", sb + j)], out=self.bank(sb + j),
                        lhsT=b["K"][ks].ap[0:dkp, kb * 128:(kb + 1) * 128], rhs=b["Q"][si].ap[0:dkp, qb * 512:(qb + 1) * 512],
                        start=True, stop=True)

            qk(0)
            pending = []
            for n in range(len(groups)):
                qb, si, g = groups[n]
                ks = pair + 2 * si
                sb = (n % 2) * 2
                pt = b["PT"][n % 3]
                if n + 1 < len(groups):
                    qk(n + 1)
                P.I("act", "activation", w=[pt.key, ("ps", sb), ("ps", sb + 1)], out=pt.ap, in_=self.bank2(sb), func=AF.Exp, scale=scale)
                acc = 4 + si
                for j in range(2):
                    kb = g * 2 + j
                    vflat = b["V"][ks].ap.rearrange("p a b -> p (a b)")
                    P.I("pe", "matmul", r=[b["V"][ks].key, pt.key], w=[("ps", acc)], out=self.bank(acc)[0:mv, :],
                        lhsT=vflat[:, kb * 65:kb * 65 + mv], rhs=pt.ap[:, j * 512:(j + 1) * 512],
                        start=(g == 0 and j == 0), stop=(g == NG - 1 and j == 1))
                if g == NG - 1:
                    xs = b["Xs"][si]
                    P.I("dve", "tensor_copy", w=[xs.key, ("ps", acc)], out=xs.ap, in_=self.bank(acc)[0:65, :])
                    if si == 1:
                        self.finalize1(b)
                        pending.append([DEFER, pair, qb])
                for pd in list(pending):
                    pd[0] -= 1
                    if pd[0] <= 0 or n == len(groups) - 1:
                        pending.remove(pd)
                        self.finalize2(b, pd[1], pd[2], o_row0)

    def finalize1(self, b):
        P = self.P
        bcol = self.bsel.ap[0:65, 0:1]
        for si in range(2):
            me, other = b["Xs"][si], b["Xs"][1 - si]
            cx, R = b["cX"][si], b["R"][si]
            P.I("dve", "scalar_tensor_tensor", r=[me.key, other.key, self.bsel.key], w=[cx.key], out=cx.ap, in0=other.ap,
                scalar=bcol, in1=me.ap, op0=ALU.mult, op1=ALU.add)
            P.I("dve", "reciprocal", r=[cx.key], w=[R.key], out=R.ap[64:65, :], in_=cx.ap[64:65, :])

    def finalize2(self, b, pair, qb, o_row0):
        P = self.P
        SLOT = self.SLOT
        for si in range(2):
            cx, R, tn, og, Gb = b["cX"][si], b["R"][si], b["tn"][si], b["og"][si], b["G"][si]
            qs = pair + 2 * si
            bc = 6 + si
            P.I("pe", "matmul", r=[R.key, self.ones_f.key], w=[("ps", bc)], out=self.bank(bc)[0:64, :],
                lhsT=self.ones_f.ap[64:65, 0:64], rhs=R.ap[64:65, :], start=True, stop=True)
            P.I("dve", "tensor_tensor", r=[cx.key], w=[tn.key, ("ps", bc)], out=tn.ap, in0=self.bank(bc)[0:64, :],
                in1=cx.ap[0:64, :], op=ALU.mult)
            P.I("pool", "tensor_tensor", r=[tn.key, Gb.key], w=[og.key], out=og.ap, in0=tn.ap,
                in1=Gb.ap[:, qb * 512:(qb + 1) * 512], op=ALU.mult)
            P.I("pool", "dma_start", r=[og.key], dma=og.key,
                out=self.OS[o_row0:o_row0 + 64, qs * SLOT + qb * 512: qs * SLOT + (qb + 1) * 512], in_=og.ap)

    def phaseB_mla(self, W, ridx):
        b = self.attn_bufs(96)
        scale = 96.0 ** -0.5
        for h in range(16):
            self.load_K(b, [(0, 64, h * 64), (64, 32, D)])
            self.load_V(b, h * 64)
            self.attention_unit(b, 96, scale, [(0, 96, h * 96)], h * 64, h * 64)

    def phaseA_gqa(self, W, xin, ridx):
        P, A = self.P, self.A
        T = self.T
        self.gains = Buf(A.alloc([128, 12], F32), "gains")
        P.I("sp", "dma_start", w=[self.gains.key], dma="gains", out=self.gains.ap[:, 0:8], in_=W["gpre"])
        P.I("sp", "dma_start", w=[self.gains.key], dma="gains", out=self.gains.ap[:, 8:12], in_=W["gqk"])
        gn = self.gains.ap
        stage = Buf(A.alloc([128, 3840], F32), "stage")
        win = Buf(A.alloc([128, 8, 3840], BF16), "win")
        for c in range(8):
            self.load_weight(win, win.ap[:, c, :], W["win"][c * 128:(c + 1) * 128, :], gn[:, c:c + 1], stage, 128)
        ones_bd = Buf(A.alloc([128, 128], BF16), "ones_bd")
        P.I("pool", "memset", w=[ones_bd.key], ap=ones_bd.ap, constant=0.0)
        P.I("pool", "memset", w=[ones_bd.key], ap=ones_bd.ap[0:64, 0:64], constant=1.0)
        P.I("pool", "memset", w=[ones_bd.key], ap=ones_bd.ap[64:128, 64:128], constant=1.0)
        pb = self.prenorm_bufs()
        rope = [Buf(A.alloc([128, 2, 512], F32), "rope") for _ in range(2)]
        tabs = Buf(A.alloc([128, 4, 512], F32), "tabs")
        sq = [Buf(A.alloc([128, 512], BF16), "sq") for _ in range(2)]
        rstd = [Buf(A.alloc([128, 512], F32), "rstd") for _ in range(2)]
        rtmp = Buf(A.alloc([128, 512], F32), "rtmp")
        t1 = [Buf(A.alloc([128, 512], F32), "t1") for _ in range(2)]
        t2 = [Buf(A.alloc([128, 512], F32), "t2") for _ in range(2)]
        qo = [Buf(A.alloc([128, 512], BF16), "qo") for _ in range(3)]
        gt = [Buf(A.alloc([128, 512], BF16), "gt") for _ in range(3)]
        vt = [Buf(A.alloc([128, 4, 256], BF16), "vt") for _ in range(2)]
        obank = [2, 3, 4, 5, 6, 7]
        ob = [0]

        def nb():
            ob[0] += 1
            return obank[ob[0] % 6]

        n = 0
        for t in range(T // 512):
            tok = slice(t * 512, (t + 1) * 512)
            hT = self.prenorm_tile(xin, t, pb)
            rp = rope[t % 2]
            P.I("sp", "dma_start", w=[rp.key], dma=rp.key, out=rp.ap, in_=W["rope"][:, :, tok].rearrange("a p t -> p a t"))
            for k in range(4):
                if k % 2 == 0:
                    P.I("dve", "tensor_scalar", r=[rp.key, self.gains.key], w=[tabs.key], out=tabs.ap[:, k, :], in0=rp.ap[:, k % 2, :],
                        scalar1=gn[:, 8 + k:9 + k], scalar2=None, op0=ALU.mult)
                else:
                    P.I("pool", "tensor_scalar", r=[rp.key, self.gains.key], w=[tabs.key], out=tabs.ap[:, k, :], in0=rp.ap[:, k % 2, :],
                        scalar1=gn[:, 8 + k:9 + k], scalar2=0.0, op0=ALU.mult, op1=ALU.add)
            for pr in range(10):
                isq = pr < 8
                c0 = pr * 128 if isq else 2048 + (pr - 8) * 128
                c1 = 1024 + pr * 128 if isq else 2304 + (pr - 8) * 128
                tk = 0 if isq else 2
                ba, bb, bs = nb(), nb(), nb()
                self.proj_fm(ba, hT, lambda c, c0=c0: win.ap[:, c, c0:c0 + 128], 8, 128, [win.key])
                self.proj_fm(bb, hT, lambda c, c1=c1: win.ap[:, c, c1:c1 + 128], 8, 128, [win.key])
                sqb, rsb, t1b, t2b, qob = sq[n % 2], rstd[n % 2], t1[n % 2], t2[n % 2], qo[n % 3]
                n += 1
                P.I("act", "activation", w=[sqb.key, ("ps", ba)], out=sqb.ap, in_=self.bank(ba), func=AF.Square)
                P.I("pe", "matmul", r=[ones_bd.key, sqb.key], w=[("ps", bs)], out=self.bank(bs), lhsT=ones_bd.ap, rhs=sqb.ap,
                    start=True, stop=True)
                self.rsqrt(rsb, rsb.ap, self.bank(bs), [("ps", bs)], 1.0 / 64.0, rtmp, rtmp.ap)
                P.I("dve", "tensor_tensor", r=[tabs.key], w=[t1b.key, ("ps", ba)], out=t1b.ap, in0=self.bank(ba),
                    in1=tabs.ap[:, tk, :], op=ALU.mult)
                P.I("dve", "tensor_tensor", r=[tabs.key], w=[t2b.key, ("ps", bb)], out=t2b.ap, in0=self.bank(bb),
                    in1=tabs.ap[:, tk + 1, :], op=ALU.mult)
                P.I("pool", "tensor_tensor", r=[t1b.key, t2b.key], w=[t1b.key], out=t1b.ap, in0=t1b.ap, in1=t2b.ap, op=ALU.add)
                P.I("pool", "tensor_tensor", r=[t1b.key, rsb.key], w=[qob.key], out=qob.ap, in0=t1b.ap, in1=rsb.ap, op=ALU.mult)
                dst = self.QS[pr * 128:(pr + 1) * 128, tok] if isq else self.KS[(pr - 8) * 128:(pr - 7) * 128, tok]
                P.I("pool", "dma_start", r=[qob.key], dma=qob.key, out=dst, in_=qob.ap)
            for g in range(8):
                bg = nb()
                gb = gt[g % 3]
                self.proj_fm(bg, hT, lambda c, g=g: win.ap[:, c, 2816 + g * 128: 2816 + (g + 1) * 128], 8, 128, [win.key])
                P.I("act", "activation", w=[gb.key, ("ps", bg)], out=gb.ap, in_=self.bank(bg), func=AF.Silu)
                P.I("pool", "dma_start", r=[gb.key], dma=gb.key, out=self.GS[g * 128:(g + 1) * 128, tok], in_=gb.ap)
            vb = vt[t % 2]
            for s in range(4):
                bv = nb()
                for c in range(8):
                    P.I("pe", "matmul", r=[win.key, hT.key], w=[("ps", bv)], out=self.bank(bv)[:, 0:256],
                        lhsT=hT.ap[:, c, s * 128:(s + 1) * 128], rhs=win.ap[:, c, 2560:2816], start=(c == 0), stop=(c == 7))
                self.evac("dve" if s % 2 == 0 else "act", vb, vb.ap[:, s, :], bv, self.bank(bv)[:, 0:256])
            P.I("pool", "dma_start", r=[vb.key], dma=vb.key, out=self.VS[tok, 0:256].rearrange("(s p) d -> p s d", p=128), in_=vb.ap)

    def phaseB_gqa(self, W, ridx):
        b = self.attn_bufs(64)
        for kvh in range(4):
            self.load_K(b, [(0, 64, kvh * 64)])
            self.load_V(b, kvh * 64)
            for g in range(4):
                h = kvh * 4 + g
                self.attention_unit(b, 64, 0.125, [(0, 64, h * 64)], h * 64, h * 64)

    def phaseA_diff(self, W, xin, ridx):
        P, A = self.P, self.A
        T = self.T
        self.gains = Buf(A.alloc([128, 12], F32), "gains")
        P.I("sp", "dma_start", w=[self.gains.key], dma="gains", out=self.gains.ap[:, 0:8], in_=W["gpre"])
        gn = self.gains.ap
        stage = Buf(A.alloc([128, 4096], F32), "stage")
        win = Buf(A.alloc([128, 8, 4096], BF16), "win")
        for c in range(8):
            self.load_weight(win, win.ap[:, c, :], W["win"][c * 128:(c + 1) * 128, :], gn[:, c:c + 1], stage, 128)
        pb = self.prenorm_bufs()
        ot = [Buf(A.alloc([128, 512], BF16), "ot") for _ in range(4)]
        vt = [Buf(A.alloc([128, 4, D], BF16), "vt") for _ in range(2)]
        obank = [2, 3, 4, 5, 6, 7]
        ob = [0]

        def nb():
            ob[0] += 1
            return obank[ob[0] % 6]

        n = 0
        for t in range(T // 512):
            tok = slice(t * 512, (t + 1) * 512)
            hT = self.prenorm_tile(xin, t, pb)
            for which, dstT, c0 in ((0, self.QS, 0), (1, self.KS, 1024), (2, self.GS, 3072)):
                for g in range(8):
                    bg = nb()
                    ob_ = ot[n % 4]
                    n += 1
                    self.proj_fm(bg, hT, lambda c, g=g, c0=c0: win.ap[:, c, c0 + g * 128: c0 + (g + 1) * 128], 8, 128, [win.key])
                    if which == 2:
                        P.I("act", "activation", w=[ob_.key, ("ps", bg)], out=ob_.ap, in_=self.bank(bg), func=AF.Silu)
                    else:
                        self.evac("dve" if g % 2 == 0 else "act", ob_, ob_.ap, bg, self.bank(bg))
                    P.I("pool", "dma_start", r=[ob_.key], dma=ob_.key, out=dstT[g * 128:(g + 1) * 128, tok], in_=ob_.ap)
            vb = vt[t % 2]
            for s in range(4):
                for hf in range(2):
                    bv = nb()
                    for c in range(8):
                        P.I("pe", "matmul", r=[win.key, hT.key], w=[("ps", bv)], out=self.bank(bv),
                            lhsT=hT.ap[:, c, s * 128:(s + 1) * 128], rhs=win.ap[:, c, 2048 + hf * 512:2048 + (hf + 1) * 512],
                            start=(c == 0), stop=(c == 7))
                    self.evac("dve" if (s + hf) % 2 == 0 else "act", vb, vb.ap[:, s, hf * 512:(hf + 1) * 512], bv, self.bank(bv))
            P.I("pool", "dma_start", r=[vb.key], dma=vb.key, out=self.VS[tok, :].rearrange("(s p) d -> p s d", p=128), in_=vb.ap)

    def phaseB_diff(self, W, ridx):
        P, A = self.P, self.A
        SLOT, NQB, NKB = self.SLOT, self.NQB, self.NKB
        NG = NKB // 2
        lam_init = 0.8 - 0.6 * math.exp(-0.3 * ridx)
        DK = 69
        lamv = Buf(A.alloc([64, 4], F32), "lamv")
        lprod = Buf(A.alloc([64, 2], F32), "lprod")
        lexp = Buf(A.alloc([64, 2], F32), "lexp")
        neglam = Buf(A.alloc([64, 1], F32), "neglam")
        subg = Buf(A.alloc([64, 2], F32), "subg")
        P.I("sp", "dma_start", w=[lamv.key], dma=lamv.key, out=lamv.ap, in_=W["lamv"])
        P.I("sp", "dma_start", w=[subg.key], dma=subg.key, out=subg.ap, in_=W["subln"])
        P.I("dve", "tensor_tensor", r=[lamv.key], w=[lprod.key], out=lprod.ap[:, 0:1], in0=lamv.ap[:, 0:1], in1=lamv.ap[:, 1:2], op=ALU.mult)
        P.I("dve", "tensor_tensor", r=[lamv.key], w=[lprod.key], out=lprod.ap[:, 1:2], in0=lamv.ap[:, 2:3], in1=lamv.ap[:, 3:4], op=ALU.mult)
        P.I("pe", "matmul", r=[lprod.key, self.ones_f.key], w=[("ps", 6)], out=self.bank(6)[0:64, 0:2], lhsT=self.ones_f.ap[0:64, 0:64],
            rhs=lprod.ap, start=True, stop=True)
        P.I("act", "activation", w=[lexp.key, ("ps", 6)], out=lexp.ap, in_=self.bank(6)[0:64, 0:2], func=AF.Exp)
        P.I("dve", "tensor_tensor", r=[lexp.key], w=[neglam.key], out=neglam.ap, in0=lexp.ap[:, 1:2], in1=lexp.ap[:, 0:1], op=ALU.subtract)
        P.I("dve", "tensor_scalar", r=[neglam.key], w=[neglam.key], out=neglam.ap, in0=neglam.ap, scalar1=-lam_init, scalar2=None, op0=ALU.add)
        P.I("dve", "tensor_scalar", r=[subg.key], w=[subg.key], out=subg.ap, in0=subg.ap, scalar1=1.0 - lam_init, scalar2=None, op0=ALU.mult)
        K = [[Buf(A.alloc([128, SLOT], BF16), "K") for _ in range(2)] for _ in range(2)]
        Kst = [Buf(A.alloc([DK, SLOT], BF16), "Kst") for _ in range(2)]
        V = [[Buf(A.alloc([128, NKB + 1, 65], BF16), "V") for _ in range(2)] for _ in range(2)]
        Vst = [Buf(A.alloc([128, NKB, 64], BF16), "Vst") for _ in range(2)]
        for g in range(2):
            for k in range(2):
                P.I("pool", "memset", w=[V[g][k].key], ap=V[g][k].ap, constant=1.0)
                P.I("pool", "memset", w=[K[g][k].key], ap=K[g][k].ap, constant=0.0)
        absd = [Buf(A.alloc([128, 2048], F32), "absd") for _ in range(2)]
        Q = [[[Buf(A.alloc([128, 512], BF16), "Q") for _ in range(3)] for _ in range(2)] for _ in range(2)]
        for si in range(2):
            for c in range(2):
                for sg in range(3):
                    P.I("pool", "memset", w=[Q[si][c][sg].key], ap=Q[si][c][sg].ap, constant=0.0)
        G = [[[Buf(A.alloc([64, 512], BF16), "G") for _ in range(2)] for _ in range(2)] for _ in range(2)]
        PT = [Buf(A.alloc([128, 1024], BF16), "PT") for _ in range(3)]
        Sb = [Buf(A.alloc([128, 1024], F32), "Sb") for _ in range(2)]
        part = [[[Buf(A.alloc([65, 512], F32), "part") for _ in range(2)] for _ in range(2)] for _ in range(2)]
        cX = [[[Buf(A.alloc([65, 512], F32), "cX") for _ in range(2)] for _ in range(2)] for _ in range(2)]
        t0 = [Buf(A.alloc([64, 512], F32), "t0") for _ in range(2)]
        Dg = [Buf(A.alloc([64, 512], F32), "Dg") for _ in range(2)]
        sqD = [Buf(A.alloc([64, 512], BF16), "sqD") for _ in range(2)]
        rstd = Buf(A.alloc([64, 512], F32), "rstdD")
        rtmp = Buf(A.alloc([64, 512], F32), "rtmpD")
        og = [Buf(A.alloc([64, 512], BF16), "ogD") for _ in range(2)]
        bcol = self.bsel.ap[0:65, 0:1]
        mv = 128 if PADV else 65

        def blend1(dst, dst_ap, own, other, np_):
            bc_, nbc_ = self.bsel.ap[0:np_, 0:1], self.bsel.ap[0:np_, 1:2]
            P.I("pool", "tensor_scalar", r=[own.key, self.bsel.key], w=[dst.key], out=dst_ap, in0=own.ap, scalar1=nbc_,
                scalar2=0.0, op0=ALU.mult, op1=ALU.add)
            P.I("dve", "scalar_tensor_tensor", r=[other.key, self.bsel.key], w=[dst.key], out=dst_ap, in0=other.ap, scalar=bc_,
                in1=dst_ap, op0=ALU.mult, op1=ALU.add)

        def fin1():
            for c in range(2):
                for si in range(2):
                    for vg in range(2):
                        me, other, cx = part[c][si][vg], part[c][1 - si][vg], cX[c][si][vg]
                        P.I("dve", "scalar_tensor_tensor", r=[me.key, other.key, self.bsel.key], w=[cx.key], out=cx.ap,
                            in0=other.ap, scalar=bcol, in1=me.ap, op0=ALU.mult, op1=ALU.add)
                    cx0 = cX[c][si][0]
                    P.I("dve", "reciprocal", r=[cx0.key], w=[cx0.key], out=cx0.ap[64:65, :], in_=cx0.ap[64:65, :])

        def fin2(h, pair, qb, par):
            for si in range(2):
                qs = pair + 2 * si
                for c in range(2):
                    cx0 = cX[c][si][0]
                    P.I("pe", "matmul", r=[cx0.key, self.ones_f.key], w=[("ps", 6 + c)], out=self.bank(6 + c)[0:64, :],
                        lhsT=self.ones_f.ap[64:65, 0:64], rhs=cx0.ap[64:65, :], start=True, stop=True)
                for vg in range(2):
                    P.I("dve", "tensor_tensor", r=[cX[0][si][vg].key], w=[t0[vg].key, ("ps", 6)], out=t0[vg].ap, in0=self.bank(6)[0:64, :],
                        in1=cX[0][si][vg].ap[0:64, :], op=ALU.mult)
                    P.I("dve", "tensor_tensor", r=[cX[1][si][vg].key], w=[Dg[vg].key, ("ps", 7)], out=Dg[vg].ap, in0=self.bank(7)[0:64, :],
                        in1=cX[1][si][vg].ap[0:64, :], op=ALU.mult)
                    P.I("dve", "scalar_tensor_tensor", r=[t0[vg].key, neglam.key], w=[Dg[vg].key], out=Dg[vg].ap, in0=Dg[vg].ap,
                        scalar=neglam.ap[:, 0:1], in1=t0[vg].ap, op0=ALU.mult, op1=ALU.add)
                    P.I("pool", "tensor_tensor", r=[Dg[vg].key], w=[sqD[vg].key], out=sqD[vg].ap, in0=Dg[vg].ap, in1=Dg[vg].ap, op=ALU.mult)
                for vg in range(2):
                    P.I("pe", "matmul", r=[sqD[vg].key, self.ones_b.key], w=[("ps", 6)], out=self.bank(6)[0:64, :],
                        lhsT=self.ones_b.ap[0:64, 0:64], rhs=sqD[vg].ap, start=(vg == 0), stop=(vg == 1))
                self.rsqrt(rstd, rstd.ap, self.bank(6)[0:64, :], [("ps", 6)], 1.0 / 128.0, rtmp, rtmp.ap)
                for vg in range(2):
                    Gt = G[par][si][vg]
                    P.I("pool", "tensor_tensor", r=[rstd.key], w=[Dg[vg].key], out=Dg[vg].ap, in0=Dg[vg].ap, in1=rstd.ap, op=ALU.mult)
                    P.I("dve", "scalar_tensor_tensor", r=[Dg[vg].key, subg.key, Gt.key], w=[og[vg].key], out=og[vg].ap,
                        in0=Dg[vg].ap, scalar=subg.ap[:, vg:vg + 1], in1=Gt.ap, op0=ALU.mult, op1=ALU.mult)
                    r0 = h * 128 + vg * 64
                    P.I("pool", "dma_start", r=[og[vg].key], dma=og[vg].key,
                        out=self.OS[r0:r0 + 64, qs * SLOT + qb * 512: qs * SLOT + (qb + 1) * 512], in_=og[vg].ap)

        for h in range(8):
            slope8 = 8.0 * 2.0 ** (-(h + 1))
            for pair in range(2):
                own = pair + 2
                oth = 5 - own
                for c in range(2):
                    r0 = h * 128 + c * 64
                    for dst, ks in ((K[c][0], pair), (Kst[0], own), (Kst[1], oth)):
                        P.I("sp", "dma_start", w=[dst.key], dma=dst.key, out=dst.ap[0:64, :], in_=self.KS[r0:r0 + 64, ks * SLOT:(ks + 1) * SLOT])
                        P.I("sp", "dma_start", w=[dst.key], dma=dst.key, out=dst.ap[64:69, :], in_=W["wrows"][h, :, ks * SLOT:(ks + 1) * SLOT])
                    blend1(K[c][1], K[c][1].ap[0:DK, :], Kst[0], Kst[1], DK)
                for g in range(2):
                    col0 = h * 128 + g * 64
                    for dst, ks in ((V[g][0], pair), (Vst[0], own), (Vst[1], oth)):
                        for k0 in range(0, NKB, 8):
                            src = self.VS[ks * SLOT + k0 * 128: ks * SLOT + (k0 + 8) * 128, col0:col0 + 64].rearrange("(k p) d -> p k d", p=128)
                            P.I("sp", "dma_start", w=[dst.key], dma=dst.key, out=dst.ap[:, k0:k0 + 8, 0:64], in_=src)
                    blend1(V[g][1], V[g][1].ap[:, 0:NKB, 0:64], Vst[0], Vst[1], 128)
                for si in range(2):
                    P.I("sp", "dma_start", w=[absd[si].key], dma=absd[si].key, out=absd[si].ap, in_=W["absd"][pair + 2 * si])
                pend = None
                for qb in range(NQB):
                    par = qb % 2
                    for si in range(2):
                        qs = pair + 2 * si
                        tq = slice(qs * SLOT + qb * 512, qs * SLOT + (qb + 1) * 512)
                        for c in range(2):
                            r0 = h * 128 + c * 64
                            for sg in range(3):
                                qt_ = Q[si][c][sg]
                                P.I("sp", "dma_start", w=[qt_.key], dma=qt_.key, out=qt_.ap[0:64, :], in_=self.QS[r0:r0 + 64, tq])
                                if sg < 2:
                                    P.I("sp", "dma_start", w=[qt_.key], dma=qt_.key, out=qt_.ap[64:69, :], in_=W["urows"][sg, :, tq])
                        for g in range(2):
                            r0 = h * 128 + g * 64
                            Gt = G[par][si][g]
                            P.I("sp", "dma_start", w=[Gt.key], dma=Gt.key, out=Gt.ap, in_=self.GS[r0:r0 + 64, tq])
                    groups = [(c, si, g) for c in range(2) for si in range(2) for g in range(NG)]

                    def is_diag(g):
                        return (g * 2) // 4 == qb

                    def qk(n):
                        c, si, g = groups[n]
                        sb = (n % 2) * 2
                        if is_diag(g):
                            sg = 2
                        else:
                            sg = 0 if (g * 2) // 4 < qb else 1
                        kk = 128 if PADK else (64 if sg == 2 else DK)
                        qt_ = Q[si][c][sg]
                        for j in range(2):
                            kb = g * 2 + j
                            P.I("pe", "matmul", r=[K[c][si].key, qt_.key], w=[("ps", sb + j)], out=self.bank(sb + j),
                                lhsT=K[c][si].ap[0:kk, kb * 128:(kb + 1) * 128], rhs=qt_.ap[0:kk, :], start=True, stop=True)

                    qk(0)
                    for n in range(len(groups)):
                        c, si, g = groups[n]
                        sb = (n % 2) * 2
                        pt = PT[n % 3]
                        if n + 1 < len(groups):
                            qk(n + 1)
                        if is_diag(g):
                            sbuf_ = Sb[n % 2]
                            j0 = (g * 2) % 4
                            P.I("dve", "scalar_tensor_tensor", r=[absd[si].key], w=[sbuf_.key, ("ps", sb), ("ps", sb + 1)], out=sbuf_.ap,
                                in0=absd[si].ap[:, j0 * 512:(j0 + 2) * 512], scalar=-slope8, in1=self.bank2(sb), op0=ALU.mult, op1=ALU.add)
                            P.I("act", "activation", r=[sbuf_.key], w=[pt.key], out=pt.ap, in_=sbuf_.ap, func=AF.Exp, scale=0.125)
                        else:
                            P.I("act", "activation", w=[pt.key, ("ps", sb), ("ps", sb + 1)], out=pt.ap, in_=self.bank2(sb), func=AF.Exp, scale=0.125)
                        for j in range(2):
                            kb = g * 2 + j
                            for vg in range(2):
                                vflat = V[vg][si].ap.rearrange("p a b -> p (a b)")
                                P.I("pe", "matmul", r=[V[vg][si].key, pt.key], w=[("ps", 4 + vg)], out=self.bank(4 + vg)[0:mv, :],
                                    lhsT=vflat[:, kb * 65:kb * 65 + mv], rhs=pt.ap[:, j * 512:(j + 1) * 512],
                                    start=(g == 0 and j == 0), stop=(g == NG - 1 and j == 1))
                        if g == NG - 1:
                            for vg in range(2):
                                self.evac("dve" if vg == 0 else "act", part[c][si][vg], part[c][si][vg].ap, 4 + vg, self.bank(4 + vg)[0:65, :])
                        if pend is not None and n == DEFER:
                            fin2(*pend)
                            pend = None
                    if pend is not None:
                        fin2(*pend)
                    fin1()
                    pend = (h, pair, qb, par)
                if pend is not None:
                    fin2(*pend)
                    pend = None

    def phaseC(self, W, xin, xout):
        P, A = self.P, self.A
        T, SLOT = self.T, self.SLOT
        stage = Buf(A.alloc([128, 1024], F32), "stageC")
        wout = Buf(A.alloc([128, 8, D], BF16), "wout")
        gpost = Buf(A.alloc([128, D], F32), "gpost")
        P.I("sp", "dma_start", w=[gpost.key], dma=gpost.key, out=gpost.ap, in_=W["gpost"].partition_broadcast(128))
        for c in range(8):
            self.load_weight(wout, wout.ap[:, c, :], W["wout"][c * 128:(c + 1) * 128, :], None, stage, 128)
        og = [Buf(A.alloc([128, 8, 512], BF16), "ogC") for _ in range(2)]
        xt = [Buf(A.alloc([128, 4, D], F32), "xtC") for _ in range(2)]
        yt = [Buf(A.alloc([128, 4, D], F32), "ytC") for _ in range(2)]
        ss = [Buf(A.alloc([128, 2], F32), "ssC") for _ in range(2)]
        rs = [Buf(A.alloc([128, 1], F32), "rsC") for _ in range(2)]
        tmp = [Buf(A.alloc([128, 1], F32), "tmpC") for _ in range(2)]
        mt = [Buf(A.alloc([128, D], F32), "mtC") for _ in range(2)]
        junk = Buf(A.alloc([128, 512], BF16), "junkC")
        n = 0
        for t in range(T // 512):
            slot, tt = divmod(t, SLOT // 512)
            tok = slice(t * 512, (t + 1) * 512)
            ogb, xb, yb = og[t % 2], xt[t % 2], yt[t % 2]
            P.I("sp", "dma_start", w=[ogb.key], dma=ogb.key, out=ogb.ap, in_=self.OS[:, tok].rearrange("(c p) t -> p c t", p=128))
            P.I("sp", "dma_start", w=[xb.key], dma=xb.key, out=xb.ap,
                in_=xin[slot, tt * 512:(tt + 1) * 512, :].rearrange("(s p) d -> p s d", p=128))
            for s in range(4):
                pb2 = (n % 2) * 2 + 4
                ssb, rsb, tmpb, mtb = ss[n % 2], rs[n % 2], tmp[n % 2], mt[n % 2]
                n += 1
                for hf in range(2):
                    for c in range(8):
                        P.I("pe", "matmul", r=[ogb.key, wout.key], w=[("ps", pb2 + hf)], out=self.bank(pb2 + hf),
                            lhsT=ogb.ap[:, c, s * 128:(s + 1) * 128], rhs=wout.ap[:, c, hf * 512:(hf + 1) * 512],
                            start=(c == 0), stop=(c == 7))
                    P.I("act", "activation", w=[junk.key, ssb.key, ("ps", pb2 + hf)], out=junk.ap, in_=self.bank(pb2 + hf),
                        func=AF.Square, accum_out=ssb.ap[:, hf:hf + 1])
                P.I("dve", "tensor_tensor", r=[ssb.key], w=[tmpb.key], out=tmpb.ap, in0=ssb.ap[:, 0:1], in1=ssb.ap[:, 1:2], op=ALU.add)
                self.rsqrt(rsb, rsb.ap, tmpb.ap, [tmpb.key], 1.0 / D, tmpb, tmpb.ap)
                for hf in range(2):
                    P.I("dve", "scalar_tensor_tensor", r=[rsb.key, gpost.key], w=[mtb.key, ("ps", pb2 + hf)],
                        out=mtb.ap[:, hf * 512:(hf + 1) * 512], in0=self.bank(pb2 + hf), scalar=rsb.ap[:, 0:1],
                        in1=gpost.ap[:, hf * 512:(hf + 1) * 512], op0=ALU.mult, op1=ALU.mult)
                P.I("pool", "tensor_tensor", r=[mtb.key, xb.key], w=[yb.key], out=yb.ap[:, s, :], in0=mtb.ap, in1=xb.ap[:, s, :], op=ALU.add)
            P.I("pool", "dma_start", r=[yb.key], dma=yb.key,
                out=xout[slot, tt * 512:(tt + 1) * 512, :].rearrange("(s p) d -> p s d", p=128), in_=yb.ap)


def rope_tab(pos, dim, theta=10000.0):
    d2 = dim // 2
    inv = 1.0 / (theta ** (np.arange(0, dim, 2, dtype=np.float32) / dim))
    ang = pos[None, :].astype(np.float32) * inv[:, None].astype(np.float32)
    c, s = np.cos(ang), np.sin(ang)
    return np.concatenate([c, c], 0).astype(np.float32), np.concatenate([-s, s], 0).astype(np.float32)


def half_perm(dim):
    d2 = dim // 2
    return np.concatenate([np.arange(d2, dim), np.arange(0, d2)])


def slot_positions(ctype, slot):
    t = np.arange(slot)
    if ctype == "P":
        return [t.copy() for _ in range(4)]
    a = (2 * (t // 512)) * 512 + t % 512
    bpos = (2 * (t // 512) + 1) * 512 + t % 512
    return [a, bpos, a, bpos]


def layer_inputs(li, kind, p, ctype, slot):
    pre = "l%d_" % li
    out = {}
    qpos = np.concatenate(slot_positions(ctype, slot))
    out[pre + "gpre"] = np.ascontiguousarray(p["pre_norm"].reshape(8, 128).T)
    out[pre + "gpost"] = np.ascontiguousarray(p["post_norm"].reshape(1, D))
    out[pre + "wout"] = np.ascontiguousarray(p["w_out"])
    if kind == "mla":
        w = p["w_in"]
        kr = w[:, 384:416]
        out[pre + "win"] = np.ascontiguousarray(np.concatenate([w[:, 0:384], kr, kr[:, half_perm(32)], w[:, 416:1440]], 1))
        wq = p["w_uq"].reshape(256, 16, 96)
        nope, ropep = wq[:, :, 0:64], wq[:, :, 64:96]
        out[pre + "wuq"] = np.ascontiguousarray(np.concatenate(
            [nope, ropep, np.zeros_like(nope), ropep[:, :, half_perm(32)]], 2).reshape(256, 3072))
        wkv = p["w_ukv"].reshape(128, 16, 128)
        out[pre + "wuk"] = np.ascontiguousarray(wkv[:, :, 0:64].reshape(128, 1024))
        out[pre + "wuv"] = np.ascontiguousarray(wkv[:, :, 64:128].reshape(128, 1024))
        out[pre + "glat"] = np.ascontiguousarray(np.stack([p["q_norm"][0:128], p["q_norm"][128:256], p["kv_norm"]], 1))
        c, s = rope_tab(qpos, 32)
        out[pre + "rope"] = np.ascontiguousarray(np.stack([c, s], 0))
    elif kind == "gqa":
        w = p["w_in"]
        perm64 = np.concatenate([half_perm(32), 32 + half_perm(32)])
        wq = w[:, 0:1024].reshape(D, 16, 64)
        wk = w[:, 1024:1280].reshape(D, 4, 64)
        out[pre + "win"] = np.ascontiguousarray(np.concatenate(
            [w[:, 0:1024], wq[:, :, perm64].reshape(D, 1024), w[:, 1024:1280], wk[:, :, perm64].reshape(D, 256),
             w[:, 1280:1536], w[:, 1536:2560]], 1))
        gq, gk = p["q_norm"], p["k_norm"]
        out[pre + "gqk"] = np.ascontiguousarray(np.stack(
            [np.tile(gq, 2), np.tile(gq[perm64], 2), np.tile(gk, 2), np.tile(gk[perm64], 2)], 1))
        rowp = (qpos // 64).astype(np.float32)
        colp = (qpos % 64).astype(np.float32)
        cr, sr = rope_tab(rowp, 32)
        cc, sc = rope_tab(colp, 32)
        c64 = np.concatenate([cr, cc], 0)
        s64 = np.concatenate([sr, sc], 0)
        out[pre + "rope"] = np.ascontiguousarray(np.stack([np.tile(c64, (2, 1)), np.tile(s64, (2, 1))], 0))
    elif kind == "diff":
        out[pre + "win"] = np.ascontiguousarray(p["w_in"])
        out[pre + "lamv"] = np.ascontiguousarray(np.stack([p["lambda_q1"], p["lambda_k1"], p["lambda_q2"], p["lambda_k2"]], 1))
        out[pre + "subln"] = np.ascontiguousarray(p["subln"].reshape(2, 64).T)
        pos = slot_positions(ctype, slot)
        ql = qpos % 512
        u = np.stack([ql % 256, 256.0 * (ql >= 256), qpos // 512, np.ones_like(qpos), np.ones_like(qpos)], 0).astype(np.float32)
        out[pre + "urows"] = np.stack([u, -u], 0).astype(NPBF)
        wr = []
        for h in range(8):
            sl = 8.0 * 2.0 ** (-(h + 1))
            wr.append(np.stack([-sl * np.ones_like(qpos), -sl * np.ones_like(qpos), -512.0 * sl * np.ones_like(qpos),
                                sl * (qpos % 128), sl * 128.0 * (qpos // 128)], 0).astype(np.float32))
        out[pre + "wrows"] = np.stack(wr, 0).astype(NPBF)
        absd = np.zeros((4, 128, 2048), np.float32)
        for seg in range(4):
            pair, si = seg % 2, seg // 2
            qs = pair + 2 * si
            if si == 0:
                ks = pair
            else:
                ks = qs if ctype == "P" else 5 - qs
            qp = pos[qs][0:512].astype(np.float32)
            kp = pos[ks][0:512].astype(np.float32).reshape(4, 128)
            for j in range(4):
                absd[seg][:, j * 512:(j + 1) * 512] = np.abs(qp[None, :] - kp[j][:, None])
        out[pre + "absd"] = absd
    return out


def core_inputs(ctype, x_slots, layers, params, slot):
    out = {"xin": np.ascontiguousarray(np.stack(x_slots, 0)).astype(np.float32)}
    b = 0.0 if ctype == "P" else 1.0
    out["bsel"] = np.tile(np.array([[b, 1.0 - b]], np.float32), (128, 1))
    out["ident"] = np.eye(128, dtype=np.float32).astype(NPBF)
    for li, (kind, ridx) in enumerate(layers):
        out.update(layer_inputs(li, kind, params[li], ctype, slot))
    return out


def split_sample(xs, slot):
    blk = xs.reshape(-1, 512, xs.shape[-1])
    a = blk[0::2].reshape(slot, -1)
    bb = blk[1::2].reshape(slot, -1)
    return [a, bb, a, bb]


def merge_sample(y, slot):
    a = y[0].reshape(-1, 512, y.shape[-1])
    bb = y[1].reshape(-1, 512, y.shape[-1])
    out = np.empty((2 * a.shape[0], 512, y.shape[-1]), y.dtype)
    out[0::2] = a
    out[1::2] = bb
    return out.reshape(2 * slot, -1)


_CACHE = {}


def run_trunk(x_prompt, x_sample, layers, params, slot):
    nP = x_prompt.shape[0] // 4
    nS = x_sample.shape[0]
    key = (slot, tuple(layers))
    if key not in _CACHE:
        bld = Builder(slot, layers)
        _CACHE[key] = (bld.build(), bld)
    nc, bld = _CACHE[key]
    in_maps = []
    for c in range(nP):
        in_maps.append(core_inputs("P", [x_prompt[4 * c + i] for i in range(4)], layers, params, slot))
    for c in range(nS):
        in_maps.append(core_inputs("S", split_sample(x_sample[c], slot), layers, params, slot))
    for m in in_maps:
        for k, (shp, dt) in bld.inputs.items():
            assert m[k].shape == shp, (k, m[k].shape, shp)
    if TRACE:
        res = run_bass_kernel_spmd(nc, in_maps, core_ids=list(range(nP + nS)), trace=True)
        print("EXEC_TIME_NS", res.exec_time_ns)
    else:
        res = run_bass_kernel_spmd(nc, in_maps, core_ids=list(range(nP + nS)))
    yp = np.stack([res.results[c]["y"][i] for c in range(nP) for i in range(4)], 0)
    ys = np.stack([merge_sample(res.results[nP + c]["y"], slot) for c in range(nS)], 0)
    return yp.astype(np.float32), ys.astype(np.float32)


FULL_LAYERS = [("mla", 0), ("gqa", 1), ("diff", 2), ("mla", 3)]


MLA_KEYS = ("pre_norm", "w_in", "q_norm", "w_uq", "kv_norm", "w_ukv", "w_out", "post_norm")
GQA_KEYS = ("pre_norm", "w_in", "q_norm", "k_norm", "w_out", "post_norm")
DIFF_KEYS = ("pre_norm", "w_in", "lambda_q1", "lambda_k1", "lambda_q2", "lambda_k2", "subln", "w_out", "post_norm")


def kernel(x_prompt, x_sample,
           l0_pre_norm, l0_w_in, l0_q_norm, l0_w_uq, l0_kv_norm, l0_w_ukv, l0_w_out, l0_post_norm,
           l1_pre_norm, l1_w_in, l1_q_norm, l1_k_norm, l1_w_out, l1_post_norm,
           l2_pre_norm, l2_w_in, l2_lambda_q1, l2_lambda_k1, l2_lambda_q2, l2_lambda_k2, l2_subln,
           l2_w_out, l2_post_norm,
           l3_pre_norm, l3_w_in, l3_q_norm, l3_w_uq, l3_kv_norm, l3_w_ukv, l3_w_out, l3_post_norm):
    f = lambda a: np.asarray(a, dtype=np.float32)
    params = [
        dict(zip(MLA_KEYS, map(f, (l0_pre_norm, l0_w_in, l0_q_norm, l0_w_uq, l0_kv_norm, l0_w_ukv, l0_w_out, l0_post_norm)))),
        dict(zip(GQA_KEYS, map(f, (l1_pre_norm, l1_w_in, l1_q_norm, l1_k_norm, l1_w_out, l1_post_norm)))),
        dict(zip(DIFF_KEYS, map(f, (l2_pre_norm, l2_w_in, l2_lambda_q1, l2_lambda_k1, l2_lambda_q2, l2_lambda_k2, l2_subln,
                                    l2_w_out, l2_post_norm)))),
        dict(zip(MLA_KEYS, map(f, (l3_pre_norm, l3_w_in, l3_q_norm, l3_w_uq, l3_kv_norm, l3_w_ukv, l3_w_out, l3_post_norm)))),
    ]
    return run_trunk(f(x_prompt), f(x_sample), FULL_LAYERS, params, 4096)
```
